# Optimizing a Trainium2 kernel written in Bass

```python
import jax, jax.numpy as jnp
from jax import lax
import numpy as np

D_MODEL = 1024
BATCH = 2
SEQ = 16384
DEPTH = 4
DEC_BATCH = 16
DEC_SEQ = 32
PAST_LEN = 2048

CHUNK = 64
Q_BLOCK = 128
N_MIXERS = 2
N_FOX = (DEPTH + 1) // 2
N_RET = DEPTH // 2
FOX_HEADS = 16
FOX_HEAD_DIM = D_MODEL // FOX_HEADS
FOX_WIDTH = FOX_HEADS * FOX_HEAD_DIM
FOX_IN_COLS = 4 * FOX_WIDTH + FOX_HEADS
RET_HEADS = 8
RET_QK_DIM = D_MODEL // RET_HEADS
RET_V_DIM = 2 * RET_QK_DIM
RET_QK_WIDTH = RET_HEADS * RET_QK_DIM
RET_V_WIDTH = RET_HEADS * RET_V_DIM
RET_IN_COLS = 2 * RET_QK_WIDTH + 2 * RET_V_WIDTH
D_FF = 2816
RMS_EPS = 1e-6
GN_EPS = 1e-5
ROPE_BASE = 10000.0

kernel_name = 'fox_retention_macaron_stream_step'

F32 = jnp.float32


def _rmsnorm(x, gain):
    xf = x.astype(F32)
    y = xf * lax.rsqrt(jnp.mean(xf * xf, axis=-1, keepdims=True) + RMS_EPS)
    return (y * gain.astype(F32)).astype(x.dtype)


def _swiglu(h, w_in, w_out):
    g, u = jnp.split(h @ w_in, 2, axis=-1)
    return (jax.nn.silu(g) * u) @ w_out


def _fox_project(h, w_in, b_f, q_gain, k_gain):
    b, t, _ = h.shape
    proj = h @ w_in
    W = FOX_WIDTH
    q = proj[..., :W].reshape(b, t, FOX_HEADS, FOX_HEAD_DIM)
    k = proj[..., W:2 * W].reshape(b, t, FOX_HEADS, FOX_HEAD_DIM)
    v = proj[..., 2 * W:3 * W].reshape(b, t, FOX_HEADS, FOX_HEAD_DIM)
    og = proj[..., 3 * W:4 * W]
    fl = proj[..., 4 * W:]
    q = _rmsnorm(q, q_gain)
    k = _rmsnorm(k, k_gain)
    logf = jax.nn.log_sigmoid((fl + b_f).astype(F32))
    return q, k, v, og, logf


def _fox_attend_prompt(q, k, v, logf):
    b, s, h, d = q.shape
    nb = s // Q_BLOCK
    cum = jnp.cumsum(logf, axis=1).swapaxes(1, 2)
    kf = k.astype(F32)
    vf = v.astype(F32)
    qb = q.astype(F32).reshape(b, nb, Q_BLOCK, h, d).swapaxes(0, 1)
    cq = cum.reshape(b, h, nb, Q_BLOCK).transpose(2, 0, 1, 3)
    kpos = jnp.arange(s)
    scale = d ** -0.5

    def block(args):
        qi, ci, i = args
        sc = jnp.einsum('bqhd,bkhd->bhqk', qi, kf) * scale + (ci[..., :, None] - cum[:, :, None, :])
        qpos = i * Q_BLOCK + jnp.arange(Q_BLOCK)
        sc = jnp.where(kpos[None, :] <= qpos[:, None], sc, -jnp.inf)
        p = jax.nn.softmax(sc, axis=-1)
        return jnp.einsum('bhqk,bkhd->bqhd', p, vf)

    o = lax.map(block, (qb, cq, jnp.arange(nb)))
    return o.swapaxes(0, 1).reshape(b, s, h, d).astype(v.dtype)


def _fox_attend_sample(q, k, v, logf, k_cache, v_cache, logf_cache):
    b, n, h, d = q.shape
    past = k_cache.shape[1]
    kall = jnp.concatenate([k_cache.astype(F32), k.astype(F32)], axis=1)
    vall = jnp.concatenate([v_cache.astype(F32), v.astype(F32)], axis=1)
    cum = jnp.cumsum(jnp.concatenate([logf_cache.astype(F32), logf], axis=1), axis=1).swapaxes(1, 2)
    sc = (jnp.einsum('bqhd,bkhd->bhqk', q.astype(F32), kall) * (d ** -0.5)
          + (cum[:, :, past:, None] - cum[:, :, None, :]))
    qpos = past + jnp.arange(n)
    kpos = jnp.arange(past + n)
    sc = jnp.where(kpos[None, :] <= qpos[:, None], sc, -jnp.inf)
    p = jax.nn.softmax(sc, axis=-1)
    return jnp.einsum('bhqk,bkhd->bqhd', p, vall).astype(v.dtype)


def _fox_output(o, og, w_out):
    b, t, _, _ = o.shape
    return (o.reshape(b, t, FOX_WIDTH) * jax.nn.sigmoid(og)) @ w_out


def _ret_log_gamma():
    return jnp.log1p(-jnp.exp2(-5.0 - jnp.arange(RET_HEADS, dtype=F32)))


def _rotary(x, pos):
    half = x.shape[-1] // 2
    inv_freq = ROPE_BASE ** (-jnp.linspace(0.0, 1.0, half, dtype=F32))
    ang = pos[:, None] * inv_freq[None, :]
    cos = jnp.cos(ang)[None, :, None, :]
    sin = jnp.sin(ang)[None, :, None, :]
    x1, x2 = x[..., :half], x[..., half:]
    return jnp.concatenate([x1 * cos - x2 * sin, x1 * sin + x2 * cos], axis=-1)


def _ret_project(h, w_in, pos):
    b, t, _ = h.shape
    proj = h @ w_in
    A, B2 = RET_QK_WIDTH, 2 * RET_QK_WIDTH
    q = proj[..., :A].reshape(b, t, RET_HEADS, RET_QK_DIM).astype(F32)
    k = proj[..., A:B2].reshape(b, t, RET_HEADS, RET_QK_DIM).astype(F32)
    v = proj[..., B2:B2 + RET_V_WIDTH].reshape(b, t, RET_HEADS, RET_V_DIM).astype(F32)
    g = proj[..., B2 + RET_V_WIDTH:]
    q = _rotary(q, pos)
    k = _rotary(k, pos) * (RET_QK_DIM ** -0.5)
    return q, k, v, g


def _ret_chunk(state, q, k, v, lg):
    n = q.shape[1]
    idx = jnp.arange(n, dtype=F32)
    dec_in = jnp.exp(idx[:, None] * lg[None, :])
    dec_out = jnp.exp((n - idx)[:, None] * lg[None, :])
    dmat = jnp.exp(jnp.abs(idx[:, None] - idx[None, :])[None] * lg[:, None, None])
    inter = jnp.einsum('bihd,bhde->bihe', q, state) * dec_in[None, :, :, None]
    sc = jnp.einsum('bihd,bjhd->bhij', q, k) * dmat[None]
    intra = jnp.einsum('bhij,bjhe->bihe', sc, v)
    new_state = (state * jnp.exp(n * lg)[None, :, None, None]
                 + jnp.einsum('bjhd,bjhe->bhde', k * dec_out[None, :, :, None], v))
    return new_state, inter + intra


def _ret_prompt(q, k, v, lg):
    b, s, h, _ = q.shape
    nc = s // CHUNK

    def to_chunks(a):
        return a.reshape(b, nc, CHUNK, *a.shape[2:]).swapaxes(0, 1)

    s0 = jnp.zeros((b, RET_HEADS, RET_QK_DIM, RET_V_DIM), F32)

    def step(st, qkv):
        return _ret_chunk(st, qkv[0], qkv[1], qkv[2], lg)

    s_fin, o = lax.scan(step, s0, (to_chunks(q), to_chunks(k), to_chunks(v)))
    return s_fin, o.swapaxes(0, 1).reshape(b, s, h, RET_V_DIM)


def _ret_output(o, g, w_out):
    b, t, _, _ = o.shape
    mu = jnp.mean(o, axis=-1, keepdims=True)
    var = jnp.mean(jnp.square(o - mu), axis=-1, keepdims=True)
    on = ((o - mu) * lax.rsqrt(var + GN_EPS)).reshape(b, t, RET_V_WIDTH).astype(g.dtype)
    return (jax.nn.silu(g) * on) @ w_out


def setup_inputs(seed: int = 0) -> dict:
    key = jax.random.key(seed)
    ks = jax.random.split(key, 17)

    def nrm(k, shape, scale):
        return jax.random.normal(k, shape, F32) * scale

    return {
        'x_prompt': nrm(ks[0], (BATCH, SEQ, D_MODEL), 1.0),
        'x_sample': nrm(ks[1], (DEC_BATCH, DEC_SEQ, D_MODEL), 1.0),
        'cache_fox_k': nrm(ks[2], (N_FOX, DEC_BATCH, PAST_LEN, FOX_HEADS, FOX_HEAD_DIM), 1.0),
        'cache_fox_v': nrm(ks[3], (N_FOX, DEC_BATCH, PAST_LEN, FOX_HEADS, FOX_HEAD_DIM), 1.0),
        'cache_fox_logf': jax.nn.log_sigmoid(jax.random.uniform(ks[4], (N_FOX, DEC_BATCH, PAST_LEN, FOX_HEADS), F32, 1.0, 4.0)),
        'state_ret': nrm(ks[5], (N_RET, DEC_BATCH, RET_HEADS, RET_QK_DIM, RET_V_DIM), 0.5),
        'norm_gains': 1.0 + nrm(ks[6], (DEPTH, 3, D_MODEL), 0.05),
        'w_ffn_in': nrm(ks[7], (DEPTH, 2, D_MODEL, 2 * D_FF), D_MODEL ** -0.5),
        'w_ffn_out': nrm(ks[8], (DEPTH, 2, D_FF, D_MODEL), D_FF ** -0.5),
        'fox_w_in': nrm(ks[9], (N_FOX, D_MODEL, FOX_IN_COLS), D_MODEL ** -0.5),
        'fox_b_f': jax.random.uniform(ks[10], (N_FOX, FOX_HEADS), F32, 1.0, 4.0),
        'fox_q_gain': 1.0 + nrm(ks[11], (N_FOX, FOX_HEAD_DIM), 0.05),
        'fox_k_gain': 1.0 + nrm(ks[12], (N_FOX, FOX_HEAD_DIM), 0.05),
        'fox_w_out': nrm(ks[13], (N_FOX, FOX_WIDTH, D_MODEL), FOX_WIDTH ** -0.5),
        'ret_w_in': nrm(ks[14], (N_RET, D_MODEL, RET_IN_COLS), D_MODEL ** -0.5),
        'ret_w_out': nrm(ks[15], (N_RET, RET_V_WIDTH, D_MODEL), RET_V_WIDTH ** -0.5),
        'final_gain': 1.0 + nrm(ks[16], (D_MODEL,), 0.05),
    }


def reference(x_prompt, x_sample, cache_fox_k, cache_fox_v, cache_fox_logf, state_ret,
              norm_gains, w_ffn_in, w_ffn_out, fox_w_in, fox_b_f, fox_q_gain, fox_k_gain,
              fox_w_out, ret_w_in, ret_w_out, final_gain):
    seq = x_prompt.shape[1]
    n_new = x_sample.shape[1]
    past = cache_fox_k.shape[2]
    pos_p = jnp.arange(seq, dtype=F32)
    pos_s = past + jnp.arange(n_new, dtype=F32)
    lg = _ret_log_gamma()
    xp, xs = x_prompt, x_sample
    fk_p, fv_p, fl_p, rs_p = [], [], [], []
    fk_s, fv_s, fl_s, rs_s = [], [], [], []
    for layer in range(DEPTH):
        g = norm_gains[layer]
        xp = xp + 0.5 * _swiglu(_rmsnorm(xp, g[0]), w_ffn_in[layer, 0], w_ffn_out[layer, 0])
        xs = xs + 0.5 * _swiglu(_rmsnorm(xs, g[0]), w_ffn_in[layer, 0], w_ffn_out[layer, 0])
        hp = _rmsnorm(xp, g[1])
        hs = _rmsnorm(xs, g[1])
        j = layer // N_MIXERS
        if layer % N_MIXERS == 0:
            q, k, v, og, lf = _fox_project(hp, fox_w_in[j], fox_b_f[j], fox_q_gain[j], fox_k_gain[j])
            xp = xp + _fox_output(_fox_attend_prompt(q, k, v, lf), og, fox_w_out[j])
            fk_p.append(k)
            fv_p.append(v)
            fl_p.append(lf.astype(xp.dtype))
            q, k, v, og, lf = _fox_project(hs, fox_w_in[j], fox_b_f[j], fox_q_gain[j], fox_k_gain[j])
            o = _fox_attend_sample(q, k, v, lf, cache_fox_k[j], cache_fox_v[j], cache_fox_logf[j])
            xs = xs + _fox_output(o, og, fox_w_out[j])
            fk_s.append(k.astype(cache_fox_k.dtype))
            fv_s.append(v.astype(cache_fox_v.dtype))
            fl_s.append(lf.astype(cache_fox_logf.dtype))
        else:
            q, k, v, gt = _ret_project(hp, ret_w_in[j], pos_p)
            st, o = _ret_prompt(q, k, v, lg)
            xp = xp + _ret_output(o, gt, ret_w_out[j])
            rs_p.append(st.astype(xp.dtype))
            q, k, v, gt = _ret_project(hs, ret_w_in[j], pos_s)
            st, o = _ret_chunk(state_ret[j].astype(F32), q, k, v, lg)
            xs = xs + _ret_output(o, gt, ret_w_out[j])
            rs_s.append(st.astype(state_ret.dtype))
        xp = xp + 0.5 * _swiglu(_rmsnorm(xp, g[2]), w_ffn_in[layer, 1], w_ffn_out[layer, 1])
        xs = xs + 0.5 * _swiglu(_rmsnorm(xs, g[2]), w_ffn_in[layer, 1], w_ffn_out[layer, 1])
    y_prompt = _rmsnorm(xp, final_gain)
    y_sample = _rmsnorm(xs, final_gain)
    return (y_prompt, y_sample, jnp.stack(fk_p), jnp.stack(fv_p), jnp.stack(fl_p), jnp.stack(rs_p),
            jnp.stack(fk_s), jnp.stack(fv_s), jnp.stack(fl_s), jnp.stack(rs_s))
```

```python
import math
from contextlib import ExitStack

import numpy as np
import concourse.bass as bass
import concourse.mybir as mybir
from concourse.bass_utils import run_bass_kernel_spmd

F32 = mybir.dt.float32
BF16 = mybir.dt.bfloat16
AF = mybir.ActivationFunctionType
ALU = mybir.AluOpType
AX = mybir.AxisListType

PE, ACT, DVE, POOL, SP = "pe", "act", "dve", "pool", "sp"
COMPUTE = (PE, ACT, DVE, POOL)

D = 1024
DFF = 2816
NH = 16
HD = 64
RH = 8
RDK = 128
RDV = 256
DEPTH = 4


class Buf:
    __slots__ = ("w", "r")

    def __init__(self):
        self.w = None
        self.r = []


class Op:
    __slots__ = ("fn", "deps", "mark", "cnt", "dma", "dsem", "dcnt", "pre")

    def __init__(self, fn, dma):
        self.fn = fn; self.deps = (); self.mark = False; self.cnt = 0
        self.dma = dma; self.dsem = None; self.dcnt = 0; self.pre = None


class Tracker:
    NDSEM = 8

    def __init__(self):
        self.ops = {e: [] for e in (PE, ACT, DVE, POOL, SP)}
        self.final = []

    def op(self, eng, fn, reads=(), writes=(), dma=False, out=False):
        lst = self.ops[eng]
        me = (eng, len(lst))
        o = Op(fn, dma)
        deps = set()
        for b in reads:
            if b.w is not None:
                deps.add(b.w)
        for b in writes:
            if b.w is not None:
                deps.add(b.w)
            deps.update(b.r)
        deps.discard(me)
        if eng == PE:
            deps = {d for d in deps if d[0] != PE}
        o.deps = tuple(deps)
        lst.append(o)
        for b in reads:
            b.r.append(me)
        for b in writes:
            b.w = me
            b.r = []
        if out:
            self.final.append(me)
        return me

    def emit(self, nc, stack):
        ops = self.ops
        dsems = {}
        for e in (SP, POOL, ACT):
            if any(o.dma for o in ops[e]):
                dsems[e] = [stack.enter_context(nc.semaphore(f"d_{e}_{i}")) for i in range(self.NDSEM)]
        for e in dsems:
            cnts = [0] * self.NDSEM
            k = 0
            for o in ops[e]:
                if o.dma:
                    s = k % self.NDSEM
                    if cnts[s] > 0:
                        o.pre = (dsems[e][s], cnts[s] * 16)
                    cnts[s] += 1
                    o.dsem = dsems[e][s]; o.dcnt = cnts[s] * 16
                    k += 1
        for e in ops:
            for o in ops[e]:
                for (de, di) in o.deps:
                    d = ops[de][di]
                    if not d.dma:
                        d.mark = True
        csem = {}
        for e in COMPUTE:
            c = 0
            for o in ops[e]:
                if o.mark and not o.dma:
                    c += 1
                    o.cnt = c
            if c:
                csem[e] = stack.enter_context(nc.semaphore(f"c_{e}"))
        block = stack.enter_context(nc.Block())
        engs = {PE: block.tensor, ACT: block.scalar, DVE: block.vector, POOL: block.gpsimd, SP: block.sync}
        final = self.final

        def make(e):
            def body(eng):
                seen = {}

                def wait(sem, val):
                    k = id(sem)
                    if seen.get(k, 0) >= val:
                        return
                    seen[k] = val
                    eng.wait_ge(sem, val)
                for o in ops[e]:
                    for (de, di) in o.deps:
                        d = ops[de][di]
                        if d.dma:
                            wait(d.dsem, d.dcnt)
                        else:
                            wait(csem[de], d.cnt)
                    if o.pre is not None:
                        wait(*o.pre)
                    ins = o.fn(eng)
                    if o.dma:
                        ins.then_inc(o.dsem, 16)
                    elif o.mark:
                        ins.then_inc(csem[e], 1)
                if e == SP:
                    for (de, di) in final:
                        d = ops[de][di]
                        wait(d.dsem, d.dcnt)
            return body
        for e in (SP, POOL, ACT, DVE, PE):
            engs[e](make(e))


class TB:
    def __init__(self, t):
        self.t = t
        self.b = Buf()


class Ring:
    def __init__(self, items):
        self.items = items
        self.i = 0

    def next(self):
        x = self.items[self.i % len(self.items)]
        self.i += 1
        return x


def build_program(SEQ, PAST, NS, SL):
    T = 512
    NT = SEQ // T
    LS = PAST + SL
    nc = bass.Bass("TRN2", target_bir_lowering=False)
    tk = Tracker()

    def dr(name, shape, kind, dt=F32):
        return nc.dram_tensor(name, list(shape), dt, kind=kind).ap()
    I, O, N = "ExternalInput", "ExternalOutput", "Internal"
    xp = dr("xp", [SEQ, D], I); xs = dr("xs", [NS, SL, D], I)
    ck = dr("ck", [2, NS, PAST, D], I); cv = dr("cv", [2, NS, PAST, D], I)
    clf = dr("clf", [2, NS, PAST, NH], I); st_in = dr("st_in", [2, NS, RH, RDK, RDV], I)
    norm_gains = dr("norm_gains", [DEPTH, 3, D], I)
    w_ffn_in = dr("w_ffn_in", [DEPTH, 2, D, 2 * DFF], I); w_ffn_out = dr("w_ffn_out", [DEPTH, 2, DFF, D], I)
    fox_w_in = dr("fox_w_in", [2, D, 4 * D + NH], I); fox_b_f = dr("fox_b_f", [2, NH], I)
    fox_q_gain = dr("fox_q_gain", [2, HD], I); fox_k_gain = dr("fox_k_gain", [2, HD], I)
    fox_w_out = dr("fox_w_out", [2, D, D], I)
    ret_w_in = dr("ret_w_in", [2, D, 6 * D], I); ret_w_out = dr("ret_w_out", [2, 2 * D, D], I)
    final_gain = dr("final_gain", [D], I)
    c_ident = dr("c_ident", [128, 128], I); c_tri = dr("c_tri", [128, 128], I)
    c_cos = dr("c_cos", [SEQ, 64], I); c_sin = dr("c_sin", [SEQ, 64], I)
    c_cos_s = dr("c_cos_s", [SL, 64], I); c_sin_s = dr("c_sin_s", [SL, 64], I)
    c_dmat = dr("c_dmat", [2, 128, RH * 128], I); c_dt0 = dr("c_dt0", [2, 128, RH * 128], I)
    c_dt1 = dr("c_dt1", [2, 128, RH * 128], I); c_dout = dr("c_dout", [2, 128, RH], I); c_gn = dr("c_gn", [2, 128, RH], I)

    yp = dr("yp", [SEQ, D], O); ys = dr("ys", [NS, SL, D], O)
    fkp = dr("fkp", [2, SEQ, D], O); fvp = dr("fvp", [2, SEQ, D], O); flp = dr("flp", [2, SEQ, NH], O)
    rsp = dr("rsp", [2, RH, RDK, RDV], O)
    fks = dr("fks", [2, NS, SL, D], O); fvs = dr("fvs", [2, NS, SL, D], O); fls = dr("fls", [2, NS, SL, NH], O)
    rss = dr("rss", [2, NS, RH, RDK, RDV], O)
    LMAX = max(SEQ, LS)
    NSTR = 1 + NS
    KTs = [[dr(f"KT_{j}_{s}", [NH, 67, LMAX], N, BF16) for s in range(NSTR)] for j in range(2)]
    VAs = [[dr(f"VA_{j}_{s}", [NH, LMAX, 65], N, BF16) for s in range(NSTR)] for j in range(2)]
    KTb = [[{} for s in range(NSTR)] for j in range(2)]
    VAb = [[{} for s in range(NSTR)] for j in range(2)]

    stack = ExitStack()
    with stack:
        def sb(name, shape, dt=F32):
            return TB(stack.enter_context(nc.sbuf_tensor(name, list(shape), dt)))

        def ps(name, shape, dt=F32):
            return TB(stack.enter_context(nc.psum_tensor(name, list(shape), dt)))

        ident_f = sb("ident_f", [128, 128]); ident_b = sb("ident_b", [128, 128], BF16)
        tri_f = sb("tri_f", [128, 128]); tri_b = sb("tri_b", [128, 128], BF16)
        ones_f = sb("ones_f", [128, 128])
        dmat = [sb("dmat", [128, RH, 128])] * 2
        dt0 = [sb("dt0", [128, RH, 128])] * 2
        dt1 = [sb("dt1", [128, RH, 128])] * 2
        dout = [sb("dout", [128, RH])] * 2
        gnt = [sb("gn", [128, RH])] * 2
        gq8 = [sb(f"gq8{j}", [128, HD]) for j in range(2)]
        gk = [sb(f"gk{j}", [128, HD]) for j in range(2)]
        bfb = [sb(f"bfb{j}", [128, NH]) for j in range(2)]
        Mcol = [sb(f"M{j}", [128, 1]) for j in range(2)]
        tmp64 = sb("tmp64", [128, HD]); tmpm = sb("tmpm", [128, 2])

        def dma(eng, out, in_, reads=(), writes=(), final=False):
            return tk.op(eng, lambda e: e.dma_start(out=out, in_=in_), reads=reads, writes=writes, dma=True, out=final)

        dma(SP, ident_f.t[:], c_ident[:, :], writes=[ident_f.b])
        dma(SP, tri_f.t[:], c_tri[:, :], writes=[tri_f.b])
        tk.op(DVE, lambda e: e.tensor_copy(ident_b.t[:], ident_f.t[:]), reads=[ident_f.b], writes=[ident_b.b])
        tk.op(DVE, lambda e: e.tensor_copy(tri_b.t[:], tri_f.t[:]), reads=[tri_f.b], writes=[tri_b.b])
        tk.op(DVE, lambda e: e.memset(ones_f.t[:], 1.0), writes=[ones_f.b])
        def load_tables(i):
            dma(SP, dmat[i].t[:].rearrange("p h c -> p (h c)"), c_dmat[i], writes=[dmat[i].b])
            dma(SP, dt0[i].t[:].rearrange("p h c -> p (h c)"), c_dt0[i], writes=[dt0[i].b])
            dma(SP, dt1[i].t[:].rearrange("p h c -> p (h c)"), c_dt1[i], writes=[dt1[i].b])
            dma(SP, dout[i].t[:], c_dout[i], writes=[dout[i].b])
            dma(SP, gnt[i].t[:], c_gn[i], writes=[gnt[i].b])
        load_tables(0)
        for j in range(2):
            dma(SP, gq8[j].t[:], fox_q_gain[j].partition_broadcast(128), writes=[gq8[j].b])
            dma(SP, gk[j].t[:], fox_k_gain[j].partition_broadcast(128), writes=[gk[j].b])
            dma(SP, bfb[j].t[:], fox_b_f[j].partition_broadcast(128), writes=[bfb[j].b])
            tk.op(DVE, lambda e, j=j: e.tensor_tensor(tmp64.t[:], gq8[j].t[:], gq8[j].t[:], ALU.mult), reads=[gq8[j].b], writes=[tmp64.b])
            tk.op(DVE, lambda e: e.tensor_reduce(tmpm.t[:, 0:1], tmp64.t[:], AX.X, ALU.max), reads=[tmp64.b], writes=[tmpm.b])
            tk.op(DVE, lambda e, j=j: e.tensor_tensor(tmp64.t[:], gk[j].t[:], gk[j].t[:], ALU.mult), reads=[gk[j].b, tmpm.b], writes=[tmp64.b])
            tk.op(DVE, lambda e: e.tensor_reduce(tmpm.t[:, 1:2], tmp64.t[:], AX.X, ALU.max), reads=[tmp64.b], writes=[tmpm.b])
            tk.op(DVE, lambda e: e.tensor_tensor(tmpm.t[:, 0:1], tmpm.t[:, 0:1], tmpm.t[:, 1:2], ALU.mult), reads=[tmpm.b], writes=[tmpm.b])
            tk.op(ACT, lambda e, j=j: e.activation(Mcol[j].t[:], tmpm.t[:, 0:1], AF.Sqrt, scale=64.0), reads=[tmpm.b], writes=[Mcol[j].b])
            tk.op(DVE, lambda e, j=j: e.tensor_scalar(gq8[j].t[:], gq8[j].t[:], 0.125, None, ALU.mult), reads=[gq8[j].b, tmp64.b], writes=[gq8[j].b])

        def view(tb_t, n, pat=None, **kw):
            v = tb_t[:, 0:n]
            return v.rearrange(pat, **kw) if pat else v

        class VB:
            def __init__(self, t):
                self.t = t
                self.b = Buf()
        x_sb = sb("x_sb", [128, 4, D])
        gbc = Ring([sb("gbc0", [128, D])])
        h_bf = sb("h_bf", [128, 4, D], BF16)
        hT = sb("hT", [128, 8, T], BF16)
        hid = sb("hid", [128, 22, T], BF16)
        sgt = Ring([sb(f"sg{i}", [128, T], BF16) for i in range(2)])
        wb = Ring([sb(f"wb{i}", [128, 4096], BF16) for i in range(2)])
        wfl = sb("wfl", [128, 8, NH], BF16)
        ssq = sb("ssq", [128, 8]); rstd = sb("rstd", [128, 8])
        junk = sb("junk", [128, D])
        stg = Ring([sb(f"stg{i}", [128, 512]) for i in range(2)])
        SLOT = 4352
        slots = [stack.enter_context(nc.sbuf_tensor(f"slot{i}", [128, SLOT], BF16)) for i in range(6)]
        q_aug = VB(view(slots[0], 4 * NH * 67, "p (b h e) -> p b h e", b=4, h=NH))
        k_aug = VB(view(slots[1], 4 * NH * 67, "p (b h e) -> p b h e", b=4, h=NH))
        v_aug = VB(view(slots[2], 4 * NH * 65, "p (b h e) -> p b h e", b=4, h=NH))
        ogs = VB(view(slots[3], 4 * D, "p (b d) -> p b d", b=4))
        ktc = Ring([VB(slots[4][:, 0:2048]), VB(slots[5][:, 0:2048])])
        vac = Ring([VB(view(slots[4][:, 2048:SLOT], 16 * 65, "p (b e) -> p b e", b=16)), VB(view(slots[5][:, 2048:SLOT], 16 * 65, "p (b e) -> p b e", b=16))])
        qrot = VB(view(slots[0], 4 * RH * 128, "p (b h e) -> p b h e", b=4, h=RH))
        krot = VB(view(slots[1], 4 * RH * 128, "p (b h e) -> p b h e", b=4, h=RH))
        kdec = VB(view(slots[2], 4 * RH * 128, "p (b h e) -> p b h e", b=4, h=RH))
        g1 = VB(view(slots[3], 4 * D, "p (b d) -> p b d", b=4))
        g2 = VB(view(slots[4], 4 * D, "p (b d) -> p b d", b=4))
        o2 = VB(view(slots[5], 4 * D, "p (b d) -> p b d", b=4))
        slot_users = [q_aug, k_aug, v_aug, ogs, qrot, krot, kdec, g1, g2, o2] + ktc.items + vac.items
        fdummy = sb("fdummy", [128, 1])

        def fence():
            tk.op(DVE, lambda e: e.memset(fdummy.t[:], 0.0), writes=[u.b for u in slot_users] + [fdummy.b])
        qTr = Ring([sb(f"qTr{i}", [128, T], BF16) for i in range(2)])
        kTr = Ring([sb(f"kTr{i}", [128, T], BF16) for i in range(2)])
        qd0r = Ring([sb(f"qd0r{i}", [128, T], BF16) for i in range(2)])
        qd1r = Ring([sb(f"qd1r{i}", [128, T], BF16) for i in range(2)])
        lf = sb("lf", [128, 4, NH]); zt = sb("zt", [128, NH]); Dtok = sb("Dtok", [128, NH])
        dr1 = sb("dr1", [128, NH])
        NBP = max(SEQ, LS) // 128 + 1
        negDM = [[sb(f"negDM{j}", [128, NBP, NH])] * NSTR for j in range(2)]
        carry = [[sb(f"carry{j}", [128, NH])] * NSTR for j in range(2)]
        pT = Ring([sb(f"pT{i}", [128, T], BF16) for i in range(3)])
        OT = sb("OT", [128, T]); rinv = sb("rinv", [128, 4])
        cs = [sb(f"cs{i}", [128, 4, 64]) for i in range(4)]
        rt = [sb(f"rt{i}", [128, 4, 64]) for i in range(2)]
        Sst = [[sb(f"S{j}", [128, RH, RDV])] * NSTR for j in range(2)]
        Sbf = [sb(f"Sbf{i}", [128, RDV], BF16) for i in range(2)]
        PTs = Ring([sb(f"PTs{i}", [128, 128], BF16) for i in range(2)])
        gst = sb("gst", [128, 4]); gtmp = sb("gtmp", [128, RDV])
        psA = Ring([ps(f"psA{i}", [128, 512]) for i in range(3)])
        psO = Ring([ps(f"psO{i}", [128, 512]) for i in range(2)])
        psTbs = [ps(f"psTb{i}", [128, 1024], BF16) for i in range(2)]
        psTf = ps("psTf", [128, 512])
        tbi = [0]
        evi = [0]

        def evac(out, in_, reads, writes):
            evi[0] += 1
            if evi[0] % 2:
                tk.op(DVE, lambda e: e.tensor_copy(out, in_), reads=reads, writes=writes)
            else:
                tk.op(ACT, lambda e: e.activation(out, in_, AF.Copy), reads=reads, writes=writes)

        def mm(out, lhsT, rhs, start, stop, reads, writes):
            tk.op(PE, lambda e: e.matmul(out, lhsT, rhs, start=start, stop=stop), reads=reads, writes=writes)

        def transpose_to(dst, dst_tb, src_fn, src_tb, R, NB, nchunk, rows=128):
            for c in range(nchunk):
                transpose_one(dst[0:rows, c, :], dst_tb, lambda b, c=c: src_fn(b, c), src_tb, R, NB, rows)

        def transpose_one(dst2, dst_tb, src_fn, src_tb, R, NB, rows=128):
            half = tbi[0] % 2
            tbi[0] += 1
            for b in range(NB):
                src = src_fn(b)
                o = psTbs[half].t[0:rows, b * 128:b * 128 + R]
                tk.op(PE, lambda e, o=o, src=src: e.transpose(o, src, ident_b.t[0:R, 0:R]),
                      reads=[src_tb.b, ident_b.b], writes=[psTbs[half].b])
            W = (NB - 1) * 128 + R
            evac(dst2[:, 0:W], psTbs[half].t[0:rows, 0:W], [psTbs[half].b], [dst_tb.b])

        def wload(src_ap, nk, ncol, reads=()):
            w = wb.next()
            view = w.t[:, 0:nk * ncol].rearrange("p (k c) -> p k c", k=nk)
            dma(POOL, view, src_ap.rearrange("(k p) c -> p k c", p=128), writes=[w.b])
            return w, view

        def rmsnorm_T(R, NB, gain_ap, nrm_out=None):
            g = gbc.next()
            dma(SP, g.t[:], gain_ap.partition_broadcast(128), writes=[g.b])
            for b in range(NB):
                tk.op(ACT, lambda e, b=b: e.activation(junk.t[0:R, :], x_sb.t[0:R, b, :], AF.Square, accum_out=ssq.t[0:R, b:b + 1]),
                      reads=[x_sb.b], writes=[junk.b, ssq.b])
            tk.op(ACT, lambda e: e.activation(rstd.t[0:R, 0:NB], ssq.t[0:R, 0:NB], AF.Sqrt, bias=1e-6, scale=1.0 / D),
                  reads=[ssq.b], writes=[rstd.b])
            tk.op(DVE, lambda e: e.reciprocal(rstd.t[0:R, 0:NB], rstd.t[0:R, 0:NB]), reads=[rstd.b], writes=[rstd.b])
            for b in range(NB):
                if nrm_out is None:
                    tk.op(DVE, lambda e, b=b: e.scalar_tensor_tensor(h_bf.t[0:R, b, :], x_sb.t[0:R, b, :], rstd.t[0:R, b:b + 1], g.t[0:R, :], ALU.mult, ALU.mult),
                          reads=[x_sb.b, rstd.b, g.b], writes=[h_bf.b])
                else:
                    tk.op(DVE, lambda e, b=b: e.scalar_tensor_tensor(x_sb.t[0:R, b, :], x_sb.t[0:R, b, :], rstd.t[0:R, b:b + 1], g.t[0:R, :], ALU.mult, ALU.mult),
                          reads=[x_sb.b, rstd.b, g.b], writes=[x_sb.b])
            if nrm_out is None:
                transpose_to(hT.t, hT, lambda b, c: h_bf.t[0:R, b, c * 128:(c + 1) * 128], h_bf, R, NB, 8)

        def add_resid(R, b, c0, ncol, ps_ap, ps_tb, scale):
            xs_ = x_sb.t[0:R, b, c0:c0 + ncol]
            tk.op(DVE, lambda e: e.scalar_tensor_tensor(xs_, ps_ap, scale, xs_, ALU.mult, ALU.add),
                  reads=[ps_tb.b, x_sb.b], writes=[x_sb.b])

        def out_proj(R, NB, TT, w_ap, nfc, scale, prep=None):
            for dh in range(2):
                pss = ([psA.next(), psA.next(), psO.next(), psO.next()])[0:NB]
                ngr = (nfc + 7) // 8
                for gi in range(ngr):
                    f0 = gi * 8
                    nf = min(8, nfc - f0)
                    w, wv = wload(w_ap[f0 * 128:(f0 + nf) * 128, dh * 512:(dh + 1) * 512], nf, 512)
                    if prep is not None:
                        prep(gi)
                    for b in range(NB):
                        for fi in range(nf):
                            fc = f0 + fi
                            mm(pss[b].t[0:R, :], hT_src[0][:, (fi if prep is not None else fc), b * 128:b * 128 + R], wv[:, fi, :], fc == 0, fc == nfc - 1,
                               [hT_src[1].b, w.b], [pss[b].b])
                for b in range(NB):
                    add_resid(R, b, dh * 512, 512, pss[b].t[0:R, :], pss[b], scale)
        hT_src = [hT.t, hT]

        def ffn(R, NB, TT, l, a):
            rmsnorm_T(R, NB, norm_gains[l, 0 if a == 0 else 2])
            wi = w_ffn_in[l, a]
            for f0 in range(0, 22, 4):
                nf = min(4, 22 - f0)
                wg, wgv = wload(wi[:, f0 * 128:(f0 + nf) * 128], 8, nf * 128)
                wu, wuv = wload(wi[:, DFF + f0 * 128:DFF + (f0 + nf) * 128], 8, nf * 128)
                for fi in range(nf):
                    pg = psA.next(); pu = psA.next()
                    for kc in range(8):
                        mm(pg.t[:, 0:TT], wgv[:, kc, fi * 128:(fi + 1) * 128], hT.t[:, kc, 0:TT], kc == 0, kc == 7, [wg.b, hT.b], [pg.b])
                    for kc in range(8):
                        mm(pu.t[:, 0:TT], wuv[:, kc, fi * 128:(fi + 1) * 128], hT.t[:, kc, 0:TT], kc == 0, kc == 7, [wu.b, hT.b], [pu.b])
                    sg = sgt.next()
                    tk.op(ACT, lambda e, sg=sg, pg=pg: e.activation(sg.t[:, 0:TT], pg.t[:, 0:TT], AF.Silu), reads=[pg.b], writes=[sg.b])
                    fc = f0 + fi
                    tk.op(DVE, lambda e, sg=sg, pu=pu, fc=fc: e.tensor_tensor(hid.t[:, fc, 0:TT], sg.t[:, 0:TT], pu.t[:, 0:TT], ALU.mult),
                          reads=[sg.b, pu.b], writes=[hid.b])
            hT_src[0], hT_src[1] = hid.t, hid
            out_proj(R, NB, TT, w_ffn_out[l, a], 22, 0.5)
            hT_src[0], hT_src[1] = hT.t, hT

        def cumsum_block(R, lf_ap, lf_tb, j, s, gblk):
            p1 = psA.next()
            mm(p1.t[0:R, 0:NH], tri_f.t[0:R, 0:R], lf_ap, True, True, [tri_f.b, lf_tb.b], [p1.b])
            p2 = psA.next()
            mm(p2.t[:, 0:NH], ones_f.t[0:R, :], lf_ap, True, True, [ones_f.b, lf_tb.b], [p2.b])
            c = carry[j][s]
            tk.op(DVE, lambda e: e.tensor_tensor(Dtok.t[0:R, :], p1.t[0:R, 0:NH], c.t[0:R, :], ALU.add), reads=[p1.b, c.b], writes=[Dtok.b])
            tk.op(DVE, lambda e: e.tensor_tensor(c.t[:], p2.t[:, 0:NH], c.t[:], ALU.add), reads=[p2.b, c.b, Dtok.b], writes=[c.b])
            nd = negDM[j][s]
            tk.op(DVE, lambda e: e.tensor_scalar(nd.t[0:R, gblk, :], Dtok.t[0:R, :], -1.0, Mcol[j].t[0:R, 0:1], ALU.mult, ALU.subtract),
                  reads=[Dtok.b, Mcol[j].b], writes=[nd.b])

        def k_head(R, NB, TT, j, s, p0, tkey, h):
            kt_ = kTr.next()
            transpose_one(kt_.t[0:67, :], kt_, lambda b: k_aug.t[0:R, b, h, :], k_aug, R, NB, rows=67)
            if tkey not in KTb[j][s]:
                KTb[j][s][tkey] = Buf()
            dma(SP, KTs[j][s][h, :, p0:p0 + TT], kt_.t[0:67, 0:TT], reads=[kt_.b], writes=[KTb[j][s][tkey]])
            return kt_

        def v_to_scratch(R, NB, TT, j, s, p0, tkey):
            VAb[j][s][tkey] = Buf()
            for b in range(NB):
                dma(SP, VAs[j][s][:, p0 + b * 128:p0 + b * 128 + R, :].rearrange("h p e -> p h e"), v_aug.t[0:R, b, :, :],
                    reads=[v_aug.b], writes=[VAb[j][s][tkey]])

        def fox(R, NB, TT, l, j, s, p0, tkey, kout, vout, lout):
            fence()
            tk.op(DVE, lambda e: e.memset(k_aug.t[:, :, :, 64:67], 1.0), writes=[k_aug.b])
            tk.op(DVE, lambda e: e.memset(v_aug.t[:, :, :, 64:65], 1.0), writes=[v_aug.b])
            rmsnorm_T(R, NB, norm_gains[l, 1])
            wi = fox_w_in[j]
            dma(POOL, wfl.t[:], wi[:, 4 * D:4 * D + NH].rearrange("(k p) c -> p k c", p=128), writes=[wfl.b])
            gb0 = p0 // 128
            for b in range(NB):
                pf = psA.next()
                for kc in range(8):
                    mm(pf.t[0:R, 0:NH], hT.t[:, kc, b * 128:b * 128 + R], wfl.t[:, kc, :], kc == 0, kc == 7, [hT.b, wfl.b], [pf.b])
                tk.op(DVE, lambda e, pf=pf: e.tensor_tensor(zt.t[0:R, :], pf.t[0:R, 0:NH], bfb[j].t[0:R, :], ALU.add), reads=[pf.b, bfb[j].b], writes=[zt.b])
                tk.op(ACT, lambda e: e.activation(zt.t[0:R, :], zt.t[0:R, :], AF.Exp, scale=-1.0), reads=[zt.b], writes=[zt.b])
                tk.op(ACT, lambda e: e.activation(zt.t[0:R, :], zt.t[0:R, :], AF.Ln, bias=1.0), reads=[zt.b], writes=[zt.b])
                tk.op(DVE, lambda e, b=b: e.tensor_scalar(lf.t[0:R, b, :], zt.t[0:R, :], -1.0, None, ALU.mult), reads=[zt.b], writes=[lf.b])
                cumsum_block(R, lf.t[0:R, b, :], lf, j, s, gb0 + b)
                tk.op(DVE, lambda e, b=b: e.tensor_copy(q_aug.t[0:R, b, :, 64], Dtok.t[0:R, :]), reads=[Dtok.b], writes=[q_aug.b])
                tk.op(DVE, lambda e, b=b: e.tensor_tensor(dr1.t[0:R, :], Dtok.t[0:R, :], q_aug.t[0:R, b, :, 64], ALU.subtract), reads=[Dtok.b, q_aug.b], writes=[dr1.b])
                tk.op(DVE, lambda e, b=b: e.tensor_copy(q_aug.t[0:R, b, :, 65], dr1.t[0:R, :]), reads=[dr1.b], writes=[q_aug.b])
                tk.op(DVE, lambda e, b=b: e.tensor_tensor(dr1.t[0:R, :], dr1.t[0:R, :], q_aug.t[0:R, b, :, 65], ALU.subtract), reads=[dr1.b, q_aug.b], writes=[dr1.b])
                tk.op(DVE, lambda e, b=b: e.tensor_copy(q_aug.t[0:R, b, :, 66], dr1.t[0:R, :]), reads=[dr1.b], writes=[q_aug.b])
            dma(SP, lout.rearrange("(b p) h -> p b h", p=R), lf.t[0:R, 0:NB, :], reads=[lf.b], final=True)
            import os
            KF = int(os.environ.get("KFOX", 9))
            if KF <= 1:
                return
            for c in range(8):
                w, wv = wload(wi[:, c * 512:(c + 1) * 512], 8, 512)
                kind, half = c // 2, c % 2
                for b in range(NB):
                    pp = psA.next()
                    for kc in range(8):
                        mm(pp.t[0:R, :], hT.t[:, kc, b * 128:b * 128 + R], wv[:, kc, :], kc == 0, kc == 7, [hT.b, w.b], [pp.b])
                    if kind in (0, 1):
                        tk.op(ACT, lambda e, pp=pp: e.activation(junk.t[0:R, 0:512], pp.t[0:R, :], AF.Square), reads=[pp.b], writes=[junk.b])
                        tk.op(DVE, lambda e: e.tensor_reduce(ssq.t[0:R, 0:8], junk.t[0:R, 0:512].rearrange("p (h d) -> p h d", h=8), AX.X, ALU.add),
                              reads=[junk.b], writes=[ssq.b])
                        tk.op(ACT, lambda e: e.activation(rstd.t[0:R, 0:8], ssq.t[0:R, 0:8], AF.Sqrt, bias=1e-6, scale=1.0 / HD), reads=[ssq.b], writes=[rstd.b])
                        tk.op(DVE, lambda e: e.reciprocal(rstd.t[0:R, 0:8], rstd.t[0:R, 0:8]), reads=[rstd.b], writes=[rstd.b])
                        tk.op(DVE, lambda e, pp=pp: e.tensor_tensor(junk.t[0:R, 0:512].rearrange("p (h d) -> p h d", h=8), pp.t[0:R, :].rearrange("p (h d) -> p h d", h=8),
                                                                  rstd.t[0:R, 0:8].unsqueeze(2).to_broadcast([R, 8, HD]), ALU.mult),
                              reads=[pp.b, rstd.b, junk.b], writes=[junk.b])
                        if kind == 0:
                            tk.op(DVE, lambda e, b=b, half=half: e.tensor_tensor(q_aug.t[0:R, b, half * 8:half * 8 + 8, 0:HD], junk.t[0:R, 0:512].rearrange("p (h d) -> p h d", h=8),
                                                                                gq8[j].t[0:R, :].unsqueeze(1).to_broadcast([R, 8, HD]), ALU.mult),
                                  reads=[junk.b, gq8[j].b], writes=[q_aug.b])
                        else:
                            sg_ = stg.next()
                            tk.op(DVE, lambda e, sg_=sg_: e.tensor_tensor(sg_.t[0:R, :].rearrange("p (h d) -> p h d", h=8), junk.t[0:R, 0:512].rearrange("p (h d) -> p h d", h=8),
                                                                        gk[j].t[0:R, :].unsqueeze(1).to_broadcast([R, 8, HD]), ALU.mult),
                                  reads=[junk.b, gk[j].b], writes=[sg_.b])
                            dma(SP, kout[b * 128:b * 128 + R, half * 512:(half + 1) * 512], sg_.t[0:R, :], reads=[sg_.b], final=True)
                            tk.op(ACT, lambda e, sg_=sg_, b=b, half=half: e.activation(k_aug.t[0:R, b, half * 8:half * 8 + 8, 0:HD], sg_.t[0:R, :].rearrange("p (h d) -> p h d", h=8), AF.Copy),
                                  reads=[sg_.b], writes=[k_aug.b])
                    elif kind == 2:
                        sg_ = stg.next()
                        tk.op(ACT, lambda e, sg_=sg_, pp=pp: e.activation(sg_.t[0:R, :], pp.t[0:R, :], AF.Copy), reads=[pp.b], writes=[sg_.b])
                        dma(SP, vout[b * 128:b * 128 + R, half * 512:(half + 1) * 512], sg_.t[0:R, :], reads=[sg_.b], final=True)
                        tk.op(DVE, lambda e, sg_=sg_, b=b, half=half: e.tensor_copy(v_aug.t[0:R, b, half * 8:half * 8 + 8, 0:HD], sg_.t[0:R, :].rearrange("p (h d) -> p h d", h=8)),
                              reads=[sg_.b], writes=[v_aug.b])
                    else:
                        tk.op(ACT, lambda e, pp=pp, b=b, half=half: e.activation(ogs.t[0:R, b, half * 512:(half + 1) * 512], pp.t[0:R, :], AF.Sigmoid), reads=[pp.b], writes=[ogs.b])
            if KF <= 2:
                return
            v_to_scratch(R, NB, TT, j, s, p0, tkey)
            if KF <= 3:
                return
            nd = negDM[j][s]
            nhist = p0 // 128
            for h in range(NH):
                qt_ = qTr.next()
                transpose_one(qt_.t[0:67, :], qt_, lambda b: q_aug.t[0:R, b, h, :], q_aug, R, NB, rows=67)
                kt_ = k_head(R, NB, TT, j, s, p0, tkey, h)
                po = psO.next()
                first = [True]
                for k0 in range(0, nhist, 16):
                    nk = min(16, nhist - k0)
                    kc_ = ktc.next(); vc_ = vac.next()
                    tkeys = sorted({(k0 * 128) // 512 + i for i in range((nk * 128 + 511) // 512)})
                    dma(SP, kc_.t[0:67, 0:nk * 128], KTs[j][s][h, :, k0 * 128:(k0 + nk) * 128], reads=[KTb[j][s][t_] for t_ in tkeys], writes=[kc_.b])
                    dma(SP, vc_.t[:, 0:nk, :], VAs[j][s][h, k0 * 128:(k0 + nk) * 128, :].rearrange("(b p) e -> p b e", p=128),
                        reads=[VAb[j][s][t_] for t_ in tkeys], writes=[vc_.b])
                    for kb in range(nk):
                        sp_ = psA.next()
                        mm(sp_.t[:, 0:TT], kc_.t[0:67, kb * 128:(kb + 1) * 128], qt_.t[0:67, 0:TT], True, True, [kc_.b, qt_.b], [sp_.b])
                        p_ = pT.next()
                        tk.op(ACT, lambda e, p_=p_, sp_=sp_, gb=k0 + kb, h=h: e.activation(p_.t[:, 0:TT], sp_.t[:, 0:TT], AF.Exp, bias=nd.t[:, gb, h:h + 1]),
                              reads=[sp_.b, nd.b], writes=[p_.b])
                        mm(po.t[0:65, 0:TT], vc_.t[:, kb, :], p_.t[:, 0:TT], first[0], False, [vc_.b, p_.b], [po.b])
                        first[0] = False
                for kb in range(NB):
                    c0 = kb * 128
                    sp_ = psA.next()
                    mm(sp_.t[0:R, c0:TT], kt_.t[0:67, c0:c0 + R], qt_.t[0:67, c0:TT], True, True, [kt_.b, qt_.b], [sp_.b])
                    p_ = pT.next()
                    tk.op(ACT, lambda e, p_=p_, sp_=sp_, gb=nhist + kb, c0=c0, h=h: e.activation(p_.t[0:R, c0:TT], sp_.t[0:R, c0:TT], AF.Exp, bias=nd.t[0:R, gb, h:h + 1]),
                          reads=[sp_.b, nd.b], writes=[p_.b])
                    tk.op(POOL, lambda e, p_=p_, c0=c0: e.tensor_tensor(p_.t[0:R, c0:c0 + R], p_.t[0:R, c0:c0 + R], tri_b.t[0:R, 0:R], ALU.mult),
                          reads=[p_.b, tri_b.b], writes=[p_.b])
                    mm(po.t[0:65, c0:TT], v_aug.t[0:R, kb, h, :], p_.t[0:R, c0:TT], first[0], kb == NB - 1, [v_aug.b, p_.b], [po.b])
                    first[0] = False
                evac(OT.t[0:65, 0:TT], po.t[0:65, 0:TT], [po.b], [OT.b])
                for b in range(NB):
                    tk.op(PE, lambda e, b=b: e.transpose(psTf.t[0:R, b * 65:(b + 1) * 65], OT.t[0:65, b * 128:b * 128 + R], ident_f.t[0:65, 0:65]),
                          reads=[OT.b, ident_f.b], writes=[psTf.b])
                pv = psTf.t[:, 0:4 * 65].rearrange("p (b e) -> p b e", e=65)
                tk.op(DVE, lambda e: e.reciprocal(rinv.t[0:R, 0:NB], pv[0:R, 0:NB, 64]), reads=[psTf.b], writes=[rinv.b])
                for b in range(NB):
                    tk.op(DVE, lambda e, b=b, h=h: e.scalar_tensor_tensor(h_bf.t[0:R, b, h * HD:(h + 1) * HD], pv[0:R, b, 0:HD], rinv.t[0:R, b:b + 1],
                                                                         ogs.t[0:R, b, h * HD:(h + 1) * HD], ALU.mult, ALU.mult),
                          reads=[psTf.b, rinv.b, ogs.b], writes=[h_bf.b])
            if KF <= 4:
                return
            transpose_to(hT.t, hT, lambda b, c: h_bf.t[0:R, b, c * 128:(c + 1) * 128], h_bf, R, NB, 8)
            out_proj(R, NB, TT, fox_w_out[j], 8, 1.0)

        def ret(R, NB, TT, l, j, s, ti, cos_ap, sin_ap):
            n = min(64, TT)
            S = Sst[j][s]
            fence()
            rmsnorm_T(R, NB, norm_gains[l, 1])
            for i, (src, scl) in enumerate(((cos_ap, None), (sin_ap, None))):
                dma(SP, cs[i].t[0:R, 0:NB, :], src.rearrange("(b p) d -> p b d", p=R), writes=[cs[i].b])
            sc_ = RDK ** -0.5
            tk.op(DVE, lambda e: e.tensor_scalar(cs[2].t[0:R, 0:NB, :], cs[0].t[0:R, 0:NB, :], sc_, None, ALU.mult), reads=[cs[0].b], writes=[cs[2].b])
            tk.op(DVE, lambda e: e.tensor_scalar(cs[3].t[0:R, 0:NB, :], cs[1].t[0:R, 0:NB, :], sc_, None, ALU.mult), reads=[cs[1].b], writes=[cs[3].b])
            wi = ret_w_in[j]
            vst = hid.t[:, 0:16, :].rearrange("p (b c) t -> p b (c t)", b=4)
            for c in range(12):
                w, wv = wload(wi[:, c * 512:(c + 1) * 512], 8, 512)
                for b in range(NB):
                    pp = psA.next()
                    for kc in range(8):
                        mm(pp.t[0:R, :], hT.t[:, kc, b * 128:b * 128 + R], wv[:, kc, :], kc == 0, kc == 7, [hT.b, w.b], [pp.b])
                    if c < 4:
                        dst = (qrot if c < 2 else krot)
                        co, si = (cs[0], cs[1]) if c < 2 else (cs[2], cs[3])
                        hh = (c % 2) * 4
                        pv4 = pp.t[0:R, :].rearrange("p (h d) -> p h d", h=4)
                        x1 = pv4[:, :, 0:64]; x2 = pv4[:, :, 64:128]
                        cb = co.t[0:R, b, :].unsqueeze(1).to_broadcast([R, 4, 64])
                        sbb = si.t[0:R, b, :].unsqueeze(1).to_broadcast([R, 4, 64])
                        r0 = rt[0].t[0:R, :, :]; r1 = rt[1].t[0:R, :, :]
                        tk.op(DVE, lambda e, x1=x1, cb=cb, r0=r0: e.tensor_tensor(r0, x1, cb, ALU.mult), reads=[pp.b, co.b], writes=[rt[0].b])
                        tk.op(DVE, lambda e, x2=x2, sbb=sbb, r1=r1: e.tensor_tensor(r1, x2, sbb, ALU.mult), reads=[pp.b, si.b], writes=[rt[1].b])
                        tk.op(DVE, lambda e, dst=dst, b=b, hh=hh, r0=r0, r1=r1: e.tensor_tensor(dst.t[0:R, b, hh:hh + 4, 0:64], r0, r1, ALU.subtract),
                              reads=[rt[0].b, rt[1].b], writes=[dst.b])
                        tk.op(DVE, lambda e, x1=x1, sbb=sbb, r0=r0: e.tensor_tensor(r0, x1, sbb, ALU.mult), reads=[pp.b, si.b], writes=[rt[0].b])
                        tk.op(DVE, lambda e, x2=x2, cb=cb, r1=r1: e.tensor_tensor(r1, x2, cb, ALU.mult), reads=[pp.b, co.b], writes=[rt[1].b])
                        tk.op(DVE, lambda e, dst=dst, b=b, hh=hh, r0=r0, r1=r1: e.tensor_tensor(dst.t[0:R, b, hh:hh + 4, 64:128], r0, r1, ALU.add),
                              reads=[rt[0].b, rt[1].b], writes=[dst.b])
                    elif c < 8:
                        tk.op(ACT, lambda e, pp=pp, b=b, c=c: e.activation(vst[0:R, b, (c - 4) * 512:(c - 3) * 512], pp.t[0:R, :], AF.Copy), reads=[pp.b], writes=[hid.b])
                    else:
                        gdst = g1 if c < 10 else g2
                        gv = gdst.t[0:R, b, (c % 2) * 512:(c % 2 + 1) * 512]
                        tk.op(ACT, lambda e, pp=pp, gv=gv: e.activation(gv, pp.t[0:R, :], AF.Silu), reads=[pp.b], writes=[gdst.b])
            ti_ = ti
            tk.op(DVE, lambda e: e.tensor_tensor(kdec.t[0:R, 0:NB, :, :], krot.t[0:R, 0:NB, :, :],
                                                 dout[ti_].t[0:R, :].unsqueeze(1).unsqueeze(3).to_broadcast([R, NB, RH, 128]), ALU.mult),
                  reads=[krot.b, dout[ti_].b], writes=[kdec.b])
            nch = 2 if n == 64 else 1
            W = (NB - 1) * 128 + R
            for h in range(RH):
                qt_ = qTr.next(); kt_ = kTr.next(); q0_ = qd0r.next(); q1_ = qd1r.next()
                transpose_one(qt_.t[:, :], qt_, lambda b: qrot.t[0:R, b, h, :], qrot, R, NB)
                transpose_one(kt_.t[:, :], kt_, lambda b: krot.t[0:R, b, h, :], krot, R, NB)
                if NB == 4:
                    qv = qt_.t[:, :].rearrange("p (b t) -> p b t", t=128)
                    tk.op(DVE, lambda e, q0_=q0_, qv=qv, h=h: e.tensor_tensor(q0_.t[:, :].rearrange("p (b t) -> p b t", t=128), qv,
                                                                             dt0[ti_].t[:, h, :].unsqueeze(1).to_broadcast([128, 4, 128]), ALU.mult),
                          reads=[qt_.b, dt0[ti_].b], writes=[q0_.b])
                    tk.op(POOL, lambda e, q1_=q1_, qv=qv, h=h: e.tensor_tensor(q1_.t[:, :].rearrange("p (b t) -> p b t", t=128), qv,
                                                                              dt1[ti_].t[:, h, :].unsqueeze(1).to_broadcast([128, 4, 128]), ALU.mult),
                          reads=[qt_.b, dt1[ti_].b], writes=[q1_.b])
                else:
                    tk.op(DVE, lambda e, q0_=q0_, qt_=qt_, h=h: e.tensor_tensor(q0_.t[:, 0:W], qt_.t[:, 0:W], dt0[ti_].t[:, h, 0:W], ALU.mult),
                          reads=[qt_.b, dt0[ti_].b], writes=[q0_.b])
                for b in range(NB):
                    po = psO.next()
                    tk.op(ACT, lambda e, h=h: e.activation(Sbf[0].t[:, :], S.t[:, h, :], AF.Copy), reads=[S.b], writes=[Sbf[0].b])
                    mm(po.t[0:R, 0:RDV], q0_.t[:, b * 128:b * 128 + R], Sbf[0].t[:, :], True, False, [q0_.b, Sbf[0].b], [po.b])
                    sp_ = psA.next()
                    mm(sp_.t[0:R, 0:R], kt_.t[:, b * 128:b * 128 + R], qt_.t[:, b * 128:b * 128 + R], True, True, [kt_.b, qt_.b], [sp_.b])
                    pt_ = PTs.next()
                    tk.op(DVE, lambda e, pt_=pt_, sp_=sp_, h=h: e.tensor_tensor(pt_.t[0:R, 0:R], sp_.t[0:R, 0:R], dmat[ti_].t[0:R, h, 0:R], ALU.mult),
                          reads=[sp_.b, dmat[ti_].b], writes=[pt_.b])
                    mm(po.t[0:R, 0:RDV], pt_.t[0:R, 0:R], vst[0:R, b, h * RDV:(h + 1) * RDV], False, nch == 1, [pt_.b, hid.b], [po.b])
                    for ch in range(nch):
                        r0_ = ch * 64
                        su = psA.next()
                        mm(su.t[:, 0:RDV], kdec.t[r0_:r0_ + n, b, h, :], vst[r0_:r0_ + n, b, h * RDV:(h + 1) * RDV], True, True, [kdec.b, hid.b], [su.b])
                        tk.op(DVE, lambda e, su=su, h=h: e.scalar_tensor_tensor(S.t[:, h, :], S.t[:, h, :], gnt[ti_].t[:, h:h + 1], su.t[:, 0:RDV], ALU.mult, ALU.add),
                              reads=[S.b, su.b, gnt[ti_].b], writes=[S.b])
                        if ch == 0 and nch == 2:
                            tk.op(ACT, lambda e, h=h: e.activation(Sbf[1].t[:, :], S.t[:, h, :], AF.Copy), reads=[S.b], writes=[Sbf[1].b])
                            mm(po.t[0:R, 0:RDV], q1_.t[:, b * 128:b * 128 + R], Sbf[1].t[:, :], False, True, [q1_.b, Sbf[1].b], [po.b])
                    tk.op(ACT, lambda e, po=po: e.activation(gtmp.t[0:R, :], po.t[0:R, 0:RDV], AF.Copy, accum_out=gst.t[0:R, 0:1]), reads=[po.b], writes=[gtmp.b, gst.b])
                    tk.op(ACT, lambda e, po=po: e.activation(gtmp.t[0:R, :], po.t[0:R, 0:RDV], AF.Square, accum_out=gst.t[0:R, 1:2]), reads=[po.b, gst.b], writes=[gtmp.b, gst.b])
                    tk.op(DVE, lambda e: e.tensor_scalar(gst.t[0:R, 0:2], gst.t[0:R, 0:2], 1.0 / RDV, None, ALU.mult), reads=[gst.b], writes=[gst.b])
                    tk.op(DVE, lambda e: e.tensor_tensor(gst.t[0:R, 2:3], gst.t[0:R, 0:1], gst.t[0:R, 0:1], ALU.mult), reads=[gst.b], writes=[gst.b])
                    tk.op(DVE, lambda e: e.tensor_tensor(gst.t[0:R, 2:3], gst.t[0:R, 1:2], gst.t[0:R, 2:3], ALU.subtract), reads=[gst.b], writes=[gst.b])
                    tk.op(ACT, lambda e: e.activation(gst.t[0:R, 3:4], gst.t[0:R, 2:3], AF.Sqrt, bias=1e-5), reads=[gst.b], writes=[gst.b])
                    tk.op(DVE, lambda e: e.reciprocal(gst.t[0:R, 3:4], gst.t[0:R, 3:4]), reads=[gst.b], writes=[gst.b])
                    gsrc = g1 if h < 4 else g2
                    goff = (h % 4) * RDV
                    gsl = gsrc.t[0:R, b, goff:goff + RDV]
                    tk.op(DVE, lambda e, po=po, gsl=gsl: e.scalar_tensor_tensor(gtmp.t[0:R, :], po.t[0:R, 0:RDV], gst.t[0:R, 0:1], gsl, ALU.subtract, ALU.mult),
                          reads=[po.b, gst.b, gsrc.b, gtmp.b], writes=[gtmp.b])
                    hdst = h_bf if h < 4 else o2
                    hv = hdst.t[0:R, b, goff:goff + RDV]
                    tk.op(DVE, lambda e, hv=hv: e.tensor_scalar(hv, gtmp.t[0:R, :], gst.t[0:R, 3:4], None, ALU.mult), reads=[gtmp.b, gst.b], writes=[hdst.b])

            def prep(gi):
                src = h_bf if gi == 0 else o2
                transpose_to(hT.t, hT, lambda b, c: src.t[0:R, b, c * 128:(c + 1) * 128], src, R, NB, 8)
            out_proj(R, NB, TT, ret_w_out[j], 16, 1.0, prep=prep)

        tk.op(DVE, lambda e: e.memset(k_aug.t[:], 1.0), writes=[k_aug.b])
        tk.op(DVE, lambda e: e.memset(v_aug.t[:], 1.0), writes=[v_aug.b])
        tk.op(DVE, lambda e: e.memset(q_aug.t[:], 0.0), writes=[q_aug.b])

        def run_tile(R, NB, TT, s, p0, tkey, ti, x_in, y_out, kouts, vouts, louts, cos_ap, sin_ap):
            dma(SP, x_sb.t[0:R, 0:NB, :], x_in.rearrange("(b p) d -> p b d", p=R), writes=[x_sb.b])
            import os
            dbg = os.environ.get("KDEBUG", "ffn,fox,ret")
            for l in range(int(os.environ.get("KDEPTH", DEPTH))):
                j = l // 2
                if "ffn" in dbg:
                    ffn(R, NB, TT, l, 0)
                if "tr" in dbg:
                    rmsnorm_T(R, NB, norm_gains[l, 0])
                if "wl" in dbg:
                    for f0 in range(0, 22, 4):
                        nf = min(4, 22 - f0)
                        wload(w_ffn_in[l, 0][:, f0 * 128:(f0 + nf) * 128], 8, nf * 128)
                if l % 2 == 0:
                    if "fox" in dbg:
                        fox(R, NB, TT, l, j, s, p0, tkey, kouts[j], vouts[j], louts[j])
                else:
                    if "ret" in dbg:
                        ret(R, NB, TT, l, j, s, ti, cos_ap, sin_ap)
                if "ffn" in dbg:
                    ffn(R, NB, TT, l, 1)
            rmsnorm_T(R, NB, final_gain, nrm_out=True)
            dma(SP, y_out.rearrange("(b p) d -> p b d", p=R), x_sb.t[0:R, 0:NB, :], reads=[x_sb.b], final=True)

        for j in range(2):
            tk.op(DVE, lambda e, j=j: e.memset(carry[j][0].t[:], 0.0), writes=[carry[j][0].b])
            tk.op(DVE, lambda e, j=j: e.memset(Sst[j][0].t[:], 0.0), writes=[Sst[j][0].b])
        for t in range(NT):
            p0 = t * T
            run_tile(128, 4, T, 0, p0, t, 0, xp[p0:p0 + T, :], yp[p0:p0 + T, :],
                     [fkp[j, p0:p0 + T, :] for j in range(2)], [fvp[j, p0:p0 + T, :] for j in range(2)],
                     [flp[j, p0:p0 + T, :] for j in range(2)], c_cos[p0:p0 + T, :], c_sin[p0:p0 + T, :])
        for j in range(2):
            dma(SP, rsp[j].rearrange("h d e -> d h e"), Sst[j][0].t[:], reads=[Sst[j][0].b], final=True)
        load_tables(1)
        import os
        for si in range(NS if not os.environ.get('KNOSAMP') else 0):
            s = 1 + si
            for j in range(2):
                tk.op(DVE, lambda e, j=j, s=s: e.memset(carry[j][s].t[:], 0.0), writes=[carry[j][s].b])
                dma(SP, Sst[j][s].t[:], st_in[j, si].rearrange("h d e -> d h e"), writes=[Sst[j][s].b])
                fence()
                tk.op(DVE, lambda e: e.memset(k_aug.t[:, :, :, 64:67], 1.0), writes=[k_aug.b])
                tk.op(DVE, lambda e: e.memset(v_aug.t[:, :, :, 64:65], 1.0), writes=[v_aug.b])
                for t in range(PAST // T):
                    p0 = t * T
                    for b in range(4):
                        for half in range(2):
                            sg_ = stg.next()
                            dma(SP, sg_.t[:], ck[j, si, p0 + b * 128:p0 + (b + 1) * 128, half * 512:(half + 1) * 512], writes=[sg_.b])
                            tk.op(DVE, lambda e, sg_=sg_, b=b, half=half: e.tensor_copy(k_aug.t[:, b, half * 8:half * 8 + 8, 0:HD], sg_.t[:].rearrange("p (h d) -> p h d", h=8)),
                                  reads=[sg_.b], writes=[k_aug.b])
                            sg_ = stg.next()
                            dma(SP, sg_.t[:], cv[j, si, p0 + b * 128:p0 + (b + 1) * 128, half * 512:(half + 1) * 512], writes=[sg_.b])
                            tk.op(DVE, lambda e, sg_=sg_, b=b, half=half: e.tensor_copy(v_aug.t[:, b, half * 8:half * 8 + 8, 0:HD], sg_.t[:].rearrange("p (h d) -> p h d", h=8)),
                                  reads=[sg_.b], writes=[v_aug.b])
                    dma(SP, lf.t[:, 0:4, :], clf[j, si, p0:p0 + T, :].rearrange("(b p) h -> p b h", p=128), writes=[lf.b])
                    for b in range(4):
                        cumsum_block(128, lf.t[:, b, :], lf, j, s, p0 // 128 + b)
                    for h in range(NH):
                        k_head(128, 4, T, j, s, p0, t, h)
                    v_to_scratch(128, 4, T, j, s, p0, t)
            run_tile(SL, 1, SL, s, PAST, PAST // T, 1, xs[si], ys[si],
                     [fks[j, si] for j in range(2)], [fvs[j, si] for j in range(2)], [fls[j, si] for j in range(2)], c_cos_s, c_sin_s)
            for j in range(2):
                dma(SP, rss[j, si].rearrange("h d e -> d h e"), Sst[j][s].t[:], reads=[Sst[j][s].b], final=True)
        tk.emit(nc, stack)
    return nc


def _tables(SEQ, PAST, SL):
    f32 = np.float32
    ident = np.eye(128, dtype=f32)
    tri = (np.arange(128)[:, None] <= np.arange(128)[None, :]).astype(f32)
    inv_freq = (f32(10000.0) ** (-np.linspace(0.0, 1.0, 64, dtype=f32))).astype(f32)

    def cs(pos):
        ang = (pos.astype(f32)[:, None] * inv_freq[None, :]).astype(f32)
        return np.cos(ang).astype(f32), np.sin(ang).astype(f32)
    cp, sp = cs(np.arange(SEQ))
    cS, sS = cs(PAST + np.arange(SL))
    lg = np.log1p(-np.exp2(-5.0 - np.arange(8, dtype=f32))).astype(f32)
    dmat = np.zeros((2, 128, 8, 128), f32); dt0 = np.zeros((2, 128, 8, 128), f32); dt1 = np.zeros((2, 128, 8, 128), f32)
    dout = np.zeros((2, 128, 8), f32); gn = np.zeros((2, 128, 8), f32)
    for ti, n in enumerate((64, SL)):
        idx = np.arange(128)
        same = (idx[:, None] // n) == (idx[None, :] // n)
        for h in range(8):
            dm = np.exp(np.abs(idx[:, None] - idx[None, :]).astype(f32) * lg[h]).astype(f32)
            dmat[ti, :, h, :] = np.where(same, dm, 0.0)
            di = np.exp((idx % n).astype(f32) * lg[h]).astype(f32)
            dt0[ti, :, h, :] = np.where(idx < n, di, 0.0)[None, :]
            dt1[ti, :, h, :] = np.where((idx >= n) & (idx < 2 * n), di, 0.0)[None, :]
            dout[ti, :, h] = np.exp((n - (idx % n)).astype(f32) * lg[h])
            gn[ti, :, h] = np.exp(f32(n) * lg[h])
    return dict(c_ident=ident, c_tri=tri, c_cos=cp, c_sin=sp, c_cos_s=cS, c_sin_s=sS,
                c_dmat=dmat.reshape(2, 128, 1024), c_dt0=dt0.reshape(2, 128, 1024), c_dt1=dt1.reshape(2, 128, 1024),
                c_dout=dout, c_gn=gn)


_CACHE = {}


def kernel(x_prompt, x_sample, cache_fox_k, cache_fox_v, cache_fox_logf, state_ret,
           norm_gains, w_ffn_in, w_ffn_out, fox_w_in, fox_b_f, fox_q_gain, fox_k_gain,
           fox_w_out, ret_w_in, ret_w_out, final_gain):
    A = lambda a: np.ascontiguousarray(np.asarray(a, dtype=np.float32))
    x_prompt, x_sample = A(x_prompt), A(x_sample)
    B, SEQ, _ = x_prompt.shape
    DB, SL, _ = x_sample.shape
    PAST = cache_fox_k.shape[2]
    import os
    NC = int(os.environ.get('KNC', 8))
    NS = DB // 8
    key = (SEQ, PAST, NS, SL)
    if key not in _CACHE:
        _CACHE[key] = build_program(SEQ, PAST, NS, SL)
    nc = _CACHE[key]
    ck = A(cache_fox_k).reshape(2, DB, PAST, D); cv = A(cache_fox_v).reshape(2, DB, PAST, D)
    clf = A(cache_fox_logf); st = A(state_ret)
    shared = dict(norm_gains=A(norm_gains), w_ffn_in=A(w_ffn_in), w_ffn_out=A(w_ffn_out), fox_w_in=A(fox_w_in),
                  fox_b_f=A(fox_b_f), fox_q_gain=A(fox_q_gain), fox_k_gain=A(fox_k_gain), fox_w_out=A(fox_w_out),
                  ret_w_in=A(ret_w_in), ret_w_out=A(ret_w_out), final_gain=A(final_gain))
    shared.update(_tables(SEQ, PAST, SL))
    in_maps = []
    for c in range(NC):
        m = dict(shared)
        m["xp"] = x_prompt[c % B]
        sl = slice(c * NS, (c + 1) * NS)
        m["xs"] = x_sample[sl]
        m["ck"] = np.ascontiguousarray(ck[:, sl]); m["cv"] = np.ascontiguousarray(cv[:, sl])
        m["clf"] = np.ascontiguousarray(clf[:, sl]); m["st_in"] = np.ascontiguousarray(st[:, sl])
        in_maps.append(m)
    res = run_bass_kernel_spmd(nc, in_maps, core_ids=list(range(NC))).results
    if NC < 8:
        return res
    y_prompt = np.stack([res[b]["yp"] for b in range(B)])
    y_sample = np.concatenate([res[c]["ys"] for c in range(NC)], 0)
    fk_p = np.stack([res[b]["fkp"] for b in range(B)], 1).reshape(2, B, SEQ, NH, HD)
    fv_p = np.stack([res[b]["fvp"] for b in range(B)], 1).reshape(2, B, SEQ, NH, HD)
    fl_p = np.stack([res[b]["flp"] for b in range(B)], 1)
    rs_p = np.stack([res[b]["rsp"] for b in range(B)], 1)
    fk_s = np.concatenate([res[c]["fks"] for c in range(NC)], 1).reshape(2, DB, SL, NH, HD)
    fv_s = np.concatenate([res[c]["fvs"] for c in range(NC)], 1).reshape(2, DB, SL, NH, HD)
    fl_s = np.concatenate([res[c]["fls"] for c in range(NC)], 1)
    rs_s = np.concatenate([res[c]["rss"] for c in range(NC)], 1)
    return (y_prompt, y_sample, fk_p, fv_p, fl_p, rs_p, fk_s, fv_s, fl_s, rs_s)
```
